# Optimizing a Trainium2 kernel written in Bass

```python
import math
import jax, jax.numpy as jnp
from jax import lax
import numpy as np

D_MODEL = 2048
BATCH = 2
SEQ = 4096
DEPTH = 1

M_HEADS = 4
M_DQK = 256
M_DV = 512
CONV_K = 4
R_HEADS = 8
R_DQK = 128
R_DV = 256
D_FF = 4 * D_MODEL
CHUNK = 128
ROPE_BASE = 10000.0
LN_EPS = 1e-5
DN_ALPHA = (2.0 * DEPTH) ** 0.25
DN_BETA = (8.0 * DEPTH) ** -0.25

M_QK_W = M_HEADS * M_DQK
M_V_W = M_HEADS * M_DV
R_QK_W = R_HEADS * R_DQK
R_V_W = R_HEADS * R_DV
SPLITS = (M_QK_W, M_QK_W, M_V_W, M_V_W, M_HEADS, M_HEADS,
          R_QK_W, R_QK_W, R_V_W, R_V_W, D_MODEL, D_MODEL)
D_IN = int(sum(SPLITS))
SPLIT_IDX = [int(c) for c in np.cumsum(SPLITS)[:-1]]

kernel_name = "hybrid_mlstm_retention_deepnorm"


def layer_norm(x, g, b):
    xf = x.astype(jnp.float32)
    mu = jnp.mean(xf, axis=-1, keepdims=True)
    var = jnp.mean(jnp.square(xf - mu), axis=-1, keepdims=True)
    y = (xf - mu) * lax.rsqrt(var + LN_EPS)
    return (y * g.astype(jnp.float32) + b.astype(jnp.float32)).astype(x.dtype)


def head_norm(h, g):
    mu = jnp.mean(h, axis=-1, keepdims=True)
    var = jnp.mean(jnp.square(h - mu), axis=-1, keepdims=True)
    return (h - mu) * lax.rsqrt(var + LN_EPS) * g.astype(jnp.float32)


def causal_dwconv(x, w, b):
    T = x.shape[1]
    xp = jnp.pad(x, ((0, 0), (CONV_K - 1, 0), (0, 0)))
    y = b
    for k in range(CONV_K):
        y = y + xp[:, k:k + T] * w[k]
    return y


def rotary(x, pos):
    half = x.shape[-1] // 2
    inv_freq = ROPE_BASE ** (-jnp.arange(half, dtype=jnp.float32) / half)
    ang = pos.astype(jnp.float32)[..., None] * inv_freq
    cos = jnp.cos(ang)[:, :, None, :]
    sin = jnp.sin(ang)[:, :, None, :]
    x1, x2 = x[..., :half], x[..., half:]
    return jnp.concatenate([x1 * cos - x2 * sin, x1 * sin + x2 * cos], axis=-1)


def to_chunks(x):
    B, T, H, d = x.shape
    return x.reshape(B, T // CHUNK, CHUNK, H, d).transpose(1, 0, 3, 2, 4)


def gate_chunks(g):
    B, T, H = g.shape
    return g.reshape(B, T // CHUNK, CHUNK, H).transpose(1, 0, 3, 2)


def from_chunks(y):
    NC, B, H, L, d = y.shape
    return y.transpose(1, 0, 3, 2, 4).reshape(B, NC * L, H, d)


def mlstm_chunkwise(q, k, v, log_i, log_f):
    B, T, H, dqk = q.shape
    dv = v.shape[-1]
    q = q * (dqk ** -0.5)
    causal = jnp.tril(jnp.ones((CHUNK, CHUNK), dtype=bool))

    def step(carry, xs):
        C, n, m = carry
        q_, k_, v_, li, lf = xs
        b = jnp.cumsum(lf, axis=-1)
        d_log = jnp.where(causal, b[..., :, None] - b[..., None, :] + li[..., None, :], -jnp.inf)
        m_inter = b + m[..., None]
        m_t = jnp.maximum(m_inter, jnp.max(d_log, axis=-1))
        w_intra = jnp.exp(d_log - m_t[..., None])
        w_inter = jnp.exp(m_inter - m_t)
        s = jnp.einsum('bhtd,bhsd->bhts', q_, k_) * w_intra
        num = (jnp.einsum('bhts,bhsv->bhtv', s, v_)
               + w_inter[..., None] * jnp.einsum('bhtd,bhdv->bhtv', q_, C))
        den = jnp.sum(s, axis=-1) + w_inter * jnp.einsum('bhtd,bhd->bht', q_, n)
        h = num / jnp.maximum(jnp.abs(den), jnp.exp(-m_t))[..., None]
        g = b[..., -1]
        w_log = g[..., None] - b + li
        m_new = jnp.maximum(g + m, jnp.max(w_log, axis=-1))
        w_s = jnp.exp(w_log - m_new[..., None])
        decay = jnp.exp(g + m - m_new)
        C = decay[..., None, None] * C + jnp.einsum('bhs,bhsd,bhsv->bhdv', w_s, k_, v_)
        n = decay[..., None] * n + jnp.einsum('bhs,bhsd->bhd', w_s, k_)
        return (C, n, m_new), h

    init = (jnp.zeros((B, H, dqk, dv), jnp.float32),
            jnp.zeros((B, H, dqk), jnp.float32),
            jnp.zeros((B, H), jnp.float32))
    xs = (to_chunks(q), to_chunks(k), to_chunks(v), gate_chunks(log_i), gate_chunks(log_f))
    _, h = lax.scan(step, init, xs)
    return from_chunks(h)


def retention_chunkwise(q, k, v, log_gamma):
    B, T, H, dk = q.shape
    dv = v.shape[-1]
    q = q * (dk ** -0.5)
    idx = jnp.arange(CHUNK, dtype=jnp.float32)
    rel = idx[:, None] - idx[None, :]
    causal = rel >= 0
    decay_intra = jnp.where(causal, jnp.exp(jnp.maximum(rel, 0.0)[None] * log_gamma[:, None, None]), 0.0)
    xi = jnp.exp((idx + 1.0)[None, :] * log_gamma[:, None])
    zeta = jnp.exp((CHUNK - 1.0 - idx)[None, :] * log_gamma[:, None])
    g_chunk = jnp.exp(CHUNK * log_gamma)

    def step(R, xs):
        q_, k_, v_ = xs
        inner = jnp.einsum('bhts,bhsv->bhtv', jnp.einsum('bhtd,bhsd->bhts', q_, k_) * decay_intra, v_)
        cross = jnp.einsum('bhtd,bhdv->bhtv', q_, R) * xi[..., None]
        R = g_chunk[:, None, None] * R + jnp.einsum('bhsd,bhsv->bhdv', k_ * zeta[..., None], v_)
        return R, inner + cross

    R0 = jnp.zeros((B, H, dk, dv), jnp.float32)
    _, o = lax.scan(step, R0, (to_chunks(q), to_chunks(k), to_chunks(v)))
    return from_chunks(o)


def hybrid_mixer(x, positions, w_in, b_in, m_conv_w, m_conv_b, m_norm_g, r_norm_g,
                 w_branch_m, w_branch_r, w_out, b_out):
    B, T, _ = x.shape
    f32 = jnp.float32
    u = jnp.einsum('btd,de->bte', x, w_in) + b_in
    mq, mk, mv, mo, mi, mf, rq, rk, rv, rg, ga, gb = jnp.split(u, SPLIT_IDX, axis=-1)

    qk = jax.nn.silu(causal_dwconv(jnp.concatenate([mq, mk], axis=-1), m_conv_w, m_conv_b))
    mq, mk = qk[..., :M_QK_W], qk[..., M_QK_W:]
    hm = mlstm_chunkwise(mq.astype(f32).reshape(B, T, M_HEADS, M_DQK),
                         mk.astype(f32).reshape(B, T, M_HEADS, M_DQK),
                         mv.astype(f32).reshape(B, T, M_HEADS, M_DV),
                         mi.astype(f32),
                         jax.nn.log_sigmoid(mf.astype(f32)))
    hm = head_norm(hm, m_norm_g) * jax.nn.sigmoid(mo.astype(f32)).reshape(B, T, M_HEADS, M_DV)
    ya = jnp.einsum('bte,ed->btd', hm.reshape(B, T, M_V_W).astype(x.dtype), w_branch_m)

    log_gamma = jnp.log1p(-(2.0 ** (-5.0 - jnp.arange(R_HEADS, dtype=f32))))
    rq4 = rotary(rq.astype(f32).reshape(B, T, R_HEADS, R_DQK), positions)
    rk4 = rotary(rk.astype(f32).reshape(B, T, R_HEADS, R_DQK), positions)
    hr = retention_chunkwise(rq4, rk4, rv.astype(f32).reshape(B, T, R_HEADS, R_DV), log_gamma)
    hr = head_norm(hr, r_norm_g) * jax.nn.silu(rg.astype(f32)).reshape(B, T, R_HEADS, R_DV)
    yb = jnp.einsum('bte,ed->btd', hr.reshape(B, T, R_V_W).astype(x.dtype), w_branch_r)

    merged = jax.nn.sigmoid(ga) * ya + jax.nn.sigmoid(gb) * yb
    return jnp.einsum('btd,de->bte', merged, w_out) + b_out


def sq_relu_mlp(x, w1, b1, w2, b2):
    h = jnp.square(jax.nn.relu(jnp.einsum('btd,df->btf', x, w1) + b1))
    return jnp.einsum('btf,fd->btd', h, w2) + b2


def setup_inputs(seed: int = 0) -> dict:
    key = jax.random.key(seed)
    ks = jax.random.split(key, 24)
    f32 = jnp.float32
    nrm = lambda k, shape, s: (jax.random.normal(k, shape, f32) * s)

    x = jax.random.normal(ks[0], (BATCH, SEQ, D_MODEL), f32)
    positions = jnp.broadcast_to(jnp.arange(SEQ, dtype=jnp.int32), (BATCH, SEQ))

    col_scale = np.ones((D_IN,), np.float32)
    offs = [0] + SPLIT_IDX + [D_IN]
    for j in (2, 8):
        col_scale[offs[j]:offs[j + 1]] = DN_BETA
    w_in = nrm(ks[1], (DEPTH, D_MODEL, D_IN), D_MODEL ** -0.5) * jnp.asarray(col_scale)
    b_in = nrm(ks[2], (DEPTH, D_IN), 0.02)
    f_lo = offs[5]
    f_bias = jnp.linspace(3.0, 6.0, M_HEADS, dtype=f32) + nrm(ks[3], (DEPTH, M_HEADS), 0.1)
    b_in = b_in.at[:, f_lo:f_lo + M_HEADS].set(f_bias)

    m_conv_w = nrm(ks[4], (DEPTH, CONV_K, 2 * M_QK_W), CONV_K ** -0.5)
    m_conv_b = nrm(ks[5], (DEPTH, 2 * M_QK_W), 0.02)
    m_norm_g = 1.0 + nrm(ks[6], (DEPTH, M_HEADS, M_DV), 0.02)
    r_norm_g = 1.0 + nrm(ks[7], (DEPTH, R_HEADS, R_DV), 0.02)
    w_branch_m = nrm(ks[8], (DEPTH, M_V_W, D_MODEL), DN_BETA * M_V_W ** -0.5)
    w_branch_r = nrm(ks[9], (DEPTH, R_V_W, D_MODEL), DN_BETA * R_V_W ** -0.5)
    w_out = nrm(ks[10], (DEPTH, D_MODEL, D_MODEL), DN_BETA * D_MODEL ** -0.5)
    b_out = nrm(ks[11], (DEPTH, D_MODEL), 0.02)
    ln1_g = 1.0 + nrm(ks[12], (DEPTH, D_MODEL), 0.02)
    ln1_b = nrm(ks[13], (DEPTH, D_MODEL), 0.02)
    w_ff1 = nrm(ks[14], (DEPTH, D_MODEL, D_FF), DN_BETA * D_MODEL ** -0.5)
    b_ff1 = nrm(ks[15], (DEPTH, D_FF), 0.02)
    w_ff2 = nrm(ks[16], (DEPTH, D_FF, D_MODEL), DN_BETA * D_FF ** -0.5)
    b_ff2 = nrm(ks[17], (DEPTH, D_MODEL), 0.02)
    ln2_g = 1.0 + nrm(ks[18], (DEPTH, D_MODEL), 0.02)
    ln2_b = nrm(ks[19], (DEPTH, D_MODEL), 0.02)
    return {"x": x, "positions": positions, "w_in": w_in, "b_in": b_in,
            "m_conv_w": m_conv_w, "m_conv_b": m_conv_b, "m_norm_g": m_norm_g, "r_norm_g": r_norm_g,
            "w_branch_m": w_branch_m, "w_branch_r": w_branch_r, "w_out": w_out, "b_out": b_out,
            "ln1_g": ln1_g, "ln1_b": ln1_b, "w_ff1": w_ff1, "b_ff1": b_ff1,
            "w_ff2": w_ff2, "b_ff2": b_ff2, "ln2_g": ln2_g, "ln2_b": ln2_b}


def reference(x, positions, w_in, b_in, m_conv_w, m_conv_b, m_norm_g, r_norm_g,
              w_branch_m, w_branch_r, w_out, b_out, ln1_g, ln1_b,
              w_ff1, b_ff1, w_ff2, b_ff2, ln2_g, ln2_b):
    h = x
    for l in range(DEPTH):
        y = hybrid_mixer(h, positions, w_in[l], b_in[l], m_conv_w[l], m_conv_b[l],
                         m_norm_g[l], r_norm_g[l], w_branch_m[l], w_branch_r[l], w_out[l], b_out[l])
        h = layer_norm(DN_ALPHA * h + y, ln1_g[l], ln1_b[l])
        f = sq_relu_mlp(h, w_ff1[l], b_ff1[l], w_ff2[l], b_ff2[l])
        h = layer_norm(DN_ALPHA * h + f, ln2_g[l], ln2_b[l])
    return h
```

```python
import math
from contextlib import ExitStack

import numpy as np
import ml_dtypes
import concourse.bass as bass
import concourse.mybir as mybir
from concourse.bass_utils import run_bass_kernel_spmd

F32 = mybir.dt.float32
BF16 = mybir.dt.bfloat16
I32 = mybir.dt.int32
AF = mybir.ActivationFunctionType
ALU = mybir.AluOpType
AX = mybir.AxisListType

COMPUTE = ("pe", "act", "dve", "pool")
ALL_ENG = ("pe", "act", "dve", "pool", "sp")


class Buf:
    def __init__(self, name, excl=False):
        self.name = name
        self.excl = excl
        self.writer = None
        self.readers = []


class Sched:
    def __init__(self, nc, stack, n_dma_sems=48, same_engine_sync=True):
        self.nc = nc
        self.prog = {e: [] for e in ALL_ENG}
        self.sem = {e: stack.enter_context(nc.semaphore(f"prog_{e}")) for e in COMPUTE}
        self.count = {e: 0 for e in COMPUTE}
        self.seen = {e: {} for e in ALL_ENG}
        self.dma_sems = [stack.enter_context(nc.semaphore(f"dma_{i}")) for i in range(n_dma_sems)]
        self.dma_val = [0] * n_dma_sems
        self.dma_next = 0
        self.same_engine_sync = same_engine_sync

    def _deps(self, eng, reads, writes):
        deps = {}

        def add(tok):
            if tok is None:
                return
            k, v = tok
            if k == eng and (eng == "pe" or not self.same_engine_sync):
                return
            if deps.get(k, 0) < v:
                deps[k] = v

        for b in reads:
            add(b.writer)
            if b.excl:
                for r in b.readers:
                    add(r)
        for b in writes:
            add(b.writer)
            for r in b.readers:
                add(r)
        waits = []
        seen = self.seen[eng]
        for k, v in deps.items():
            if seen.get(k, 0) >= v:
                continue
            seen[k] = v
            waits.append((k, v))
        return waits

    def _mark(self, tok, reads, writes):
        for b in reads:
            if b.excl:
                b.writer = tok
                b.readers = []
            else:
                b.readers.append(tok)
                if len(b.readers) > 96:
                    b.readers = b.readers[-64:]
        for b in writes:
            b.writer = tok
            b.readers = []

    def op(self, eng, fn, reads=(), writes=(), inc=True):
        waits = self._deps(eng, reads, writes)
        if inc:
            self.count[eng] += 1
            tok = (eng, self.count[eng])
        else:
            tok = (eng, self.count[eng] + 1)
        self._mark(tok, reads, writes)
        self.prog[eng].append((waits, fn, ("inc", eng) if inc else None))
        return tok

    def dma(self, queue, fn, reads=(), writes=(), amount=16):
        waits = self._deps(queue, reads, writes)
        i = self.dma_next
        self.dma_next = (self.dma_next + 1) % len(self.dma_sems)
        key = ("dma", i)
        if self.dma_val[i] > 0 and self.seen[queue].get(key, 0) < self.dma_val[i]:
            self.seen[queue][key] = self.dma_val[i]
            waits.append((key, self.dma_val[i]))
        self.dma_val[i] += amount
        tok = (key, self.dma_val[i])
        self._mark(tok, reads, writes)
        self.prog[queue].append((waits, fn, ("dma", i, amount)))
        return tok

    def barrier(self):
        for e in ALL_ENG:
            waits = []
            for k in COMPUTE:
                if k != e and self.count[k] > 0 and self.seen[e].get(k, 0) < self.count[k]:
                    self.seen[e][k] = self.count[k]
                    waits.append((k, self.count[k]))
            for i, v in enumerate(self.dma_val):
                key = ("dma", i)
                if v > 0 and self.seen[e].get(key, 0) < v:
                    self.seen[e][key] = v
                    waits.append((key, v))
            self.prog[e].append((waits, None, None))

    def wait_all(self, eng, bufs):
        waits = self._deps(eng, list(bufs), list(bufs))
        self.prog[eng].append((waits, None, None))

    def _handle(self, key):
        return self.dma_sems[key[1]] if isinstance(key, tuple) else self.sem[key]

    def replay(self, name, eng):
        for waits, fn, inc in self.prog[name]:
            for k, v in waits:
                eng.wait_ge(self._handle(k), v)
            if fn is None:
                continue
            ins = fn(eng)
            if inc is not None:
                if inc[0] == "inc":
                    ins.then_inc(self.sem[inc[1]], 1)
                else:
                    ins.then_inc(self.dma_sems[inc[1]], inc[2])

    def run(self):
        with self.nc.Block() as block:
            @block.tensor
            def _(e):
                self.replay("pe", e)

            @block.scalar
            def _(e):
                self.replay("act", e)

            @block.vector
            def _(e):
                self.replay("dve", e)

            @block.gpsimd
            def _(e):
                self.replay("pool", e)

            @block.sync
            def _(e):
                self.replay("sp", e)


class PsumRing:
    def __init__(self, nc, stack, n=8):
        self.banks = [stack.enter_context(nc.psum_tensor(f"ps{i}", [128, 512], F32)) for i in range(n)]
        self.bufs = [Buf(f"ps{i}", excl=True) for i in range(n)]
        self.i = 0

    def next(self):
        b, u = self.banks[self.i], self.bufs[self.i]
        self.i = (self.i + 1) % len(self.banks)
        return b, u


D = 2048
T = 4096
L = 128
NCH = T // L
KT = D // 128
DFF = 8192
EPS = 1e-5
ALPHA = 2.0 ** 0.25
TWO_PI = 2.0 * math.pi
QK, MV, MO, RQK, RV, RG, IFG = 0, 512, 1024, 1536, 2048, 2560, 3072
NCA = 3074
TQ = 1024
NTT = TQ // 128
NWB = 64 + 16 + 128


def mm(S, out, lhsT, rhs, start, stop, reads, writes, inc):
    S.op("pe", lambda e: e.matmul(out, lhsT=lhsT, rhs=rhs, start=start, stop=stop),
         reads=reads, writes=writes, inc=inc)


def tr(S, out, in_, ident, reads, writes, inc):
    S.op("pe", lambda e: e.transpose(out, in_, ident), reads=reads, writes=writes, inc=inc)


def act(S, out, in_, func, reads, writes, bias=0.0, scale=1.0):
    S.op("act", lambda e: e.activation(out=out, in_=in_, func=func, bias=bias, scale=scale),
         reads=reads, writes=writes)


def tt(S, eng, out, in0, in1, op, reads, writes):
    S.op(eng, lambda e: e.tensor_tensor(out=out, in0=in0, in1=in1, op=op), reads=reads, writes=writes)


def ts(S, eng, out, in0, s1, s2, op0, op1, reads, writes):
    if s2 is None:
        S.op(eng, lambda e: e.tensor_scalar(out=out, in0=in0, scalar1=s1, scalar2=None, op0=op0),
             reads=reads, writes=writes)
    else:
        S.op(eng, lambda e: e.tensor_scalar(out=out, in0=in0, scalar1=s1, scalar2=s2, op0=op0, op1=op1),
             reads=reads, writes=writes)


def stt(S, out, in0, scalar, in1, op0, op1, reads, writes):
    S.op("dve", lambda e: e.scalar_tensor_tensor(out=out, in0=in0, scalar=scalar, in1=in1, op0=op0, op1=op1),
         reads=reads, writes=writes)


def cp(S, eng, out, in_, reads, writes):
    if eng == "act":
        S.op("act", lambda e: e.copy(out=out, in_=in_), reads=reads, writes=writes)
    else:
        S.op(eng, lambda e: e.tensor_copy(out=out, in_=in_), reads=reads, writes=writes)


def dma(S, q, out, in_, reads, writes):
    S.dma(q, lambda e: e.dma_start(out=out, in_=in_), reads=reads, writes=writes)


def layer_norm_rows(S, x_ap, xb, width, stats, mv, sd, rstd, nmr, sb):
    nst = width // 512
    for i in range(nst):
        S.op("dve", lambda e, i=i: e.bn_stats(out=stats[:, i * 6:(i + 1) * 6], in_=x_ap[:, i * 512:(i + 1) * 512]),
             reads=[xb], writes=[sb])
    S.op("dve", lambda e: e.bn_aggr(out=mv[:, 0:2], in_=stats[:, 0:6 * nst]), reads=[sb], writes=[sb])
    act(S, sd[:, 0:1], mv[:, 1:2], AF.Sqrt, [sb], [sb], bias=EPS)
    S.op("dve", lambda e: e.reciprocal(out=rstd[:, 0:1], in_=sd[:, 0:1]), reads=[sb], writes=[sb])
    ts(S, "dve", nmr[:, 0:1], mv[:, 0:1], -1.0, rstd[:, 0:1], ALU.mult, ALU.mult, [sb], [sb])
    act(S, x_ap, x_ap, AF.Identity, [xb, sb], [xb], bias=nmr[:, 0:1], scale=rstd[:, 0:1])


def build_program(part, debug=False, stop_after='all', nch_run=NCH):
    nc = bass.Bass("TRN2", target_bir_lowering=False)

    def din(name, shape, dt=F32):
        return nc.dram_tensor(name, list(shape), dt, kind="ExternalInput").ap()

    cmat = din("cmat", [128, 512])
    NWD = NWB
    NWS = 0
    if part == 'A':
        xb = din("xb", [T, D])
        pos = din("pos", [128, NCH], I32)
        wA = din("wA", [D, NCA])
        bA_tm = din("bA_tm", [1, 2560])
        colsA = din("colsA", [128, 40])
        mng = din("mng", [1, 512])
        rng = din("rng", [1, 512])
        invf = din("invf", [128, 64])
        hx = nc.dram_tensor("hx", [1024, T], BF16, kind="ExternalOutput").ap()

        def ex_dst(c, half):
            return hx[half * 512:half * 512 + 512, c * 128:(c + 1) * 128].rearrange("(j p) t -> p j t", p=128)
    else:
        xq = din("xq", [TQ, D])
        hmhr = din("hmhr", [4096, TQ], BF16)
        wBd = din("wBd", [NWD * 128, 2048])
        colsB = din("colsB", [128, 96])
        rowsB = din("rowsB", [6, D])
        out = nc.dram_tensor("out", [TQ, D], F32, kind="ExternalOutput").ap()
        def wB_block(i):
            return wBd[i * 128:(i + 1) * 128, :]

    with ExitStack() as st:
        S = Sched(nc, st)
        ring = PsumRing(nc, st)

        ARENA_F32 = 53000
        arena = st.enter_context(nc.sbuf_tensor("arena", [128, ARENA_F32], F32))

        class Bump:
            def __init__(self, start):
                self.off = start

            def __call__(self, name, shape, dt=F32):
                n = 1
                for d_ in shape[1:]:
                    n *= d_
                nf = n if dt != BF16 else (n + 1) // 2
                nf = (nf + 7) // 8 * 8
                ap = arena[:, self.off:self.off + nf]
                self.off += nf
                assert self.off <= ARENA_F32, (name, self.off)
                if dt == BF16:
                    ap = ap.bitcast(BF16)[:, 0:n]
                elif dt == I32:
                    ap = ap.bitcast(I32)[:, 0:n]
                else:
                    ap = ap[:, 0:n]
                if len(shape) == 3:
                    ap = ap.rearrange("p (a b) -> p a b", a=shape[1])
                return ap

        sb = Bump(0)

        cm = sb("cm", [128, 512])
        cols = sb("cols", [128, 40])
        ident_f = cm[:, 0:128]
        mask16 = cm[:, 128:256]
        mask01 = cm[:, 256:384]
        ones_f = cm[:, 384:512]
        cmb = sb("cmb", [128, 256], BF16)
        ident_b = cmb[:, 0:128]
        ones_b = cmb[:, 128:256]
        Bc = Buf("consts")
        dma(S, "sp", cm[:], cmat, [], [Bc])
        if part == 'A':
            dma(S, "sp", cols[:], colsA, [], [Bc])
        cp(S, "dve", cmb[:, 0:128], cm[:, 0:128], [Bc], [Bc])
        cp(S, "dve", cmb[:, 128:256], cm[:, 384:512], [Bc], [Bc])
        bqk = cols[:, 0:4]
        bif = cols[:, 4:6]
        convw = cols[:, 6:22]
        convb = cols[:, 22:26]
        rconst = cols[:, 26:32]

        xc = [sb(f"xc{i}", [128, D]) for i in range(2)]
        colsb = sb("colsb", [128, 96])
        lnst = sb("lnst", [128, 40])
        COMMON_END = sb.off
        if part == 'A':
            Wt = sb("Wt", [128, KT, NCA], BF16)
            BW = {}
            for name_, off, n in (("QK", QK, 512), ("MV", MV, 512), ("MO", MO, 512), ("RQK", RQK, 512),
                                  ("RV", RV, 512), ("RG", RG, 512), ("IF", IFG, 2)):
                BW[off] = Buf("W" + name_)
                S.dma("pool", lambda e, off=off, n=n: e.dma_start(
                    out=Wt[:, :, off:off + n], in_=wA[:, off:off + n].rearrange("(k p) n -> p k n", p=128)),
                    writes=[BW[off]])
            btm = sb("btm", [128, 2560])
            gmn = sb("gmn", [128, 512])
            grn = sb("grn", [128, 512])
            invf_t = sb("invf_t", [128, 64])
            pos_i = sb("pos_i", [128, NCH], I32)
            dma(S, "sp", btm[:], bA_tm.broadcast_to([128, 2560]), [], [Bc])
            dma(S, "sp", gmn[:], mng.broadcast_to([128, 512]), [], [Bc])
            dma(S, "sp", grn[:], rng.broadcast_to([128, 512]), [], [Bc])
            dma(S, "sp", invf_t[:], invf, [], [Bc])
            dma(S, "sp", pos_i[:], pos, [], [Bc])

            cosT = sb("cosT", [128, NCH * 64])
            sinT = sb("sinT", [128, NCH * 64])
            RP = 256
            rt_a = sb("rt_a", [128, RP])
            rt_b = sb("rt_b", [128, RP])
            rt_i = sb("rt_i", [128, RP], I32)
            posf = sb("posf", [128, NCH])
            Brt = Buf("rt")
            cp(S, "dve", posf[:], pos_i[:], [Bc], [Brt])
            C1 = float(np.float32(6.28125))
            C2 = float(TWO_PI - 6.28125)
            PI_LO = 3.1415925
            for pc in range(NCH * 64 // RP):
                for cc in range(RP // 64):
                    c = pc * (RP // 64) + cc
                    ts(S, "dve", rt_a[:, cc * 64:(cc + 1) * 64], invf_t[:], posf[:, c:c + 1], None, ALU.mult, None,
                       [Bc, Brt], [Brt])
                ts(S, "dve", rt_b[:], rt_a[:], 1.0 / TWO_PI, None, ALU.mult, None, [Brt], [Brt])
                cp(S, "dve", rt_i[:], rt_b[:], [Brt], [Brt])
                cp(S, "dve", rt_b[:], rt_i[:], [Brt], [Brt])
                stt(S, rt_a[:], rt_b[:], -C1, rt_a[:], ALU.mult, ALU.add, [Brt], [Brt])
                stt(S, rt_a[:], rt_b[:], -C2, rt_a[:], ALU.mult, ALU.add, [Brt], [Brt])
                ts(S, "dve", rt_b[:], rt_a[:], -math.pi, TWO_PI, ALU.is_lt, ALU.mult, [Brt], [Brt])
                tt(S, "dve", rt_a[:], rt_a[:], rt_b[:], ALU.add, [Brt], [Brt])
                ts(S, "dve", rt_b[:], rt_a[:], math.pi, -TWO_PI, ALU.is_gt, ALU.mult, [Brt], [Brt])
                tt(S, "dve", rt_a[:], rt_a[:], rt_b[:], ALU.add, [Brt], [Brt])
                ts(S, "dve", rt_b[:], rt_a[:], math.pi / 2, -TWO_PI, ALU.is_gt, ALU.mult, [Brt], [Brt])
                stt(S, rt_b[:], rt_a[:], math.pi / 2, rt_b[:], ALU.add, ALU.add, [Brt], [Brt])
                ts(S, "dve", rt_a[:], rt_a[:], -PI_LO, PI_LO, ALU.max, ALU.min, [Brt], [Brt])
                ts(S, "dve", rt_b[:], rt_b[:], -PI_LO, PI_LO, ALU.max, ALU.min, [Brt], [Brt])
                act(S, sinT[:, pc * RP:(pc + 1) * RP], rt_a[:], AF.Sin, [Brt], [Brt])
                act(S, cosT[:, pc * RP:(pc + 1) * RP], rt_b[:], AF.Sin, [Brt], [Brt])

            Bxc = [Buf(f"xc{i}") for i in range(2)]
            xT1 = sb("xT", [128, KT, 128], BF16)
            xT = [xT1, xT1]
            BxT1 = Buf("xT")
            BxT = [BxT1, BxT1]
            pre = [sb(f"pre{i}", [128, 4, 131]) for i in range(2)]
            Bpre = [Buf(f"pre{i}") for i in range(2)]
            cv = sb("cv", [128, 4, 128])
            Bcv = Buf("cv")
            qkT = [sb(f"qkT{i}", [128, 4, 128], BF16) for i in range(2)]
            BqkT = [Buf(f"qkT{i}") for i in range(2)]
            vm = [sb(f"vm{i}", [128, 512], BF16) for i in range(2)]
            Bvm = [Buf(f"vm{i}") for i in range(2)]
            gm = [sb(f"gm{i}", [128, 512]) for i in range(2)]
            Bgm = [Buf(f"gm{i}") for i in range(2)]
            rqp = sb("rqp", [128, 512])
            Brqp = Buf("rqp")
            rot = sb("rot", [128, 512])
            Brot = Buf("rot")
            rtmp = sb("rtmp", [128, 4, 256])
            Brtmp = Buf("rtmp")
            rqs = [sb(f"rqs{i}", [128, 4, 128], BF16) for i in range(2)]
            Brqs = [Buf(f"rqs{i}") for i in range(2)]
            rv = [sb(f"rv{i}", [128, 512], BF16) for i in range(2)]
            Brv = [Buf(f"rv{i}") for i in range(2)]
            grg = [sb(f"grg{i}", [128, 512]) for i in range(2)]
            Bgrg = [Buf(f"grg{i}") for i in range(2)]
            gcol = sb("gcol", [128, 8])
            Bg = Buf("gates")
            diag = sb("diag", [128, 128])
            gstat = sb("gstat", [128, 8])
            Gw = sb("Gw", [128, NCH * 4])
            BGw = [Buf(f"Gw{c}") for c in range(NCH)]
            Cst = sb("Cst", [128, 2, 512])
            nst = sb("nst", [128, 2])
            Cbf = sb("Cbf", [128, 2, 512], BF16)
            nbf = sb("nbf", [128, 2], BF16)
            BC, BCbf = Buf("C"), Buf("Cbf")
            sTw = sb("sTw", [128, 128], BF16)
            kw = sb("kw", [128, 256], BF16)
            BsTw, Bkw = Buf("sTw"), Buf("kw")
            hbuf = sb("hbuf", [128, 512])
            Bh = Buf("hbuf")
            hm = sb("hm", [128, 512], BF16)
            Bhm = Buf("hm")
            hoT = [sb(f"hoT{i}", [128, 4, 128], BF16) for i in range(2)]
            BhoT = [Buf(f"hoT{i}") for i in range(2)]
            small = sb("small", [128, 32])
            Bsm = Buf("small")
            Zst = sb("Zst", [128, 2, 256])
            Zbf = sb("Zbf", [128, 2, 256], BF16)
            BZ, BZbf = Buf("Z"), Buf("Zbf")
            rT = sb("rT", [128, 4, 128], BF16)
            BrT = Buf("rT")
            sTm = sb("sTm", [128, 2, 128], BF16)
            BsTm = Buf("sTm")
            obuf = sb("obuf", [128, 512])
            Bo = Buf("obuf")
            hr = sb("hr", [128, 512], BF16)
            Bhr = Buf("hr")
            hrT = [sb(f"hrT{i}", [128, 4, 128], BF16) for i in range(2)]
            BhrT = [Buf(f"hrT{i}") for i in range(2)]
            rsm = sb("rsm", [128, 32])
            Brsm = Buf("rsm")
            Bex = Buf("exin")

            S.op("dve", lambda e: e.memset(Cst[:], 0.0), writes=[BC])
            S.op("dve", lambda e: e.memset(nst[:], 0.0), writes=[BC])
            S.op("dve", lambda e: e.memset(Zst[:], 0.0), writes=[BZ])
            S.op("dve", lambda e: e.memset(gcol[:], 0.0), writes=[Bg])
            S.op("dve", lambda e: e.memset(pre[1][:], 0.0), writes=[Bpre[1]])

            def load_x(c):
                dma(S, "sp", xc[c % 2][:], xb[c * 128:(c + 1) * 128, :], [], [Bxc[c % 2]])

            def stage1(c):
                p = c % 2
                for j in range(4):
                    bank, bb = ring.next()
                    for i in range(4):
                        k = 4 * j + i
                        tr(S, bank[:, i * 128:(i + 1) * 128], xc[p][:, k * 128:(k + 1) * 128], ident_f,
                           [Bxc[p], Bc], [bb], inc=(i == 3))
                    dst = xT[p][:, 4 * j:4 * j + 4, :].rearrange("p k n -> p (k n)")
                    if j % 2 == 0:
                        cp(S, "act", dst, bank[:], [bb], [BxT[p]])
                    else:
                        cp(S, "dve", dst, bank[:], [bb], [BxT[p]])
                if c + 2 < nch_run:
                    load_x(c + 2)
                bank, bb = ring.next()
                for m in range(4):
                    for k in range(KT):
                        mm(S, bank[:, m * 128:(m + 1) * 128], Wt[:, k, QK + m * 128:QK + (m + 1) * 128], xT[p][:, k, :],
                           k == 0, k == KT - 1, [BxT[p], BW[QK]], [bb], inc=(m == 3 and k == KT - 1))
                for m in range(4):
                    act(S, pre[p][:, m, 3:131], bank[:, m * 128:(m + 1) * 128], AF.Identity, [bb, Bc], [Bpre[p]],
                        bias=bqk[:, m:m + 1])
                bank2, bb2 = ring.next()
                for k in range(KT):
                    mm(S, bank2[:, 0:2], xT[p][:, k, :], Wt[:, k, IFG:IFG + 2], k == 0, k == KT - 1,
                       [BxT[p], BW[IFG]], [bb2], inc=(k == KT - 1))
                tt(S, "dve", gcol[:, 5:7], bank2[:, 0:2], bif, ALU.add, [bb2, Bc], [Bg])
                def proj(off):
                    bank_, bb_ = ring.next()
                    for k in range(KT):
                        mm(S, bank_[:], xT[p][:, k, :], Wt[:, k, off:off + 512], k == 0, k == KT - 1,
                           [BxT[p], BW[off]], [bb_], inc=(k == KT - 1))
                    return bank_, bb_
                bank_, bb_ = proj(MV)
                tt(S, "dve", vm[p][:], bank_[:], btm[:, 0:512], ALU.add, [bb_, Bc], [Bvm[p]])
                bank_, bb_ = proj(MO)
                tt(S, "dve", gm[p][:], bank_[:], btm[:, 512:1024], ALU.add, [bb_, Bc], [Bgm[p]])
                act(S, gm[p][:], gm[p][:], AF.Sigmoid, [Bgm[p]], [Bgm[p]])
                tt(S, "pool", gm[p][:], gm[p][:], gmn[:], ALU.mult, [Bgm[p], Bc], [Bgm[p]])
                bank_, bb_ = proj(RQK)
                tt(S, "dve", rqp[:], bank_[:], btm[:, 1024:1536], ALU.add, [bb_, Bc], [Brqp])
                bank_, bb_ = proj(RV)
                tt(S, "dve", rv[p][:], bank_[:], btm[:, 1536:2048], ALU.add, [bb_, Bc], [Brv[p]])
                bank_, bb_ = proj(RG)
                tt(S, "dve", grg[p][:], bank_[:], btm[:, 2048:2560], ALU.add, [bb_, Bc], [Bgrg[p]])
                act(S, grg[p][:], grg[p][:], AF.Silu, [Bgrg[p]], [Bgrg[p]])
                tt(S, "pool", grg[p][:], grg[p][:], grn[:], ALU.mult, [Bgrg[p], Bc], [Bgrg[p]])
                cp(S, "pool", pre[p][:, :, 0:3], pre[1 - p][:, :, 128:131], [Bpre[1 - p]], [Bpre[p]])
                for m in range(4):
                    ts(S, "dve", cv[:, m, :], pre[p][:, m, 0:128], convw[:, m * 4:m * 4 + 1], convb[:, m:m + 1],
                       ALU.mult, ALU.add, [Bpre[p], Bc], [Bcv])
                    for tap in range(1, 4):
                        stt(S, cv[:, m, :], pre[p][:, m, tap:tap + 128], convw[:, m * 4 + tap:m * 4 + tap + 1], cv[:, m, :],
                            ALU.mult, ALU.add, [Bpre[p], Bc, Bcv], [Bcv])
                act(S, qkT[p][:].rearrange("p a b -> p (a b)"), cv[:].rearrange("p a b -> p (a b)"), AF.Silu,
                    [Bcv], [BqkT[p]])
                rq4 = rqp[:].rearrange("p (v h d) -> p v h d", v=4, h=2)
                ro4 = rot[:].rearrange("p (v h d) -> p v h d", v=4, h=2)
                cosb = cosT[:, c * 64:(c + 1) * 64].unsqueeze(1).to_broadcast([128, 4, 64])
                sinb = sinT[:, c * 64:(c + 1) * 64].unsqueeze(1).to_broadcast([128, 4, 64])
                t1, t2, t3, t4 = rtmp[:, :, 0:64], rtmp[:, :, 64:128], rtmp[:, :, 128:192], rtmp[:, :, 192:256]
                tt(S, "dve", t1, rq4[:, :, 0, :], cosb, ALU.mult, [Brqp, Brt], [Brtmp])
                tt(S, "dve", t2, rq4[:, :, 1, :], sinb, ALU.mult, [Brqp, Brt], [Brtmp])
                tt(S, "dve", ro4[:, :, 0, :], t1, t2, ALU.subtract, [Brtmp], [Brot])
                tt(S, "pool", t3, rq4[:, :, 0, :], sinb, ALU.mult, [Brqp, Brt], [Brtmp])
                tt(S, "pool", t4, rq4[:, :, 1, :], cosb, ALU.mult, [Brqp, Brt], [Brtmp])
                tt(S, "pool", ro4[:, :, 1, :], t3, t4, ALU.add, [Brtmp], [Brot])
                for v in range(4):
                    act(S, rqs[p][:, v, :], rot[:, v * 128:(v + 1) * 128], AF.Copy, [Brot, Bc], [Brqs[p]],
                        scale=rconst[:, v:v + 1])
                act(S, gcol[:, 0:1], gcol[:, 6:7], AF.Exp, [Bg], [Bg], scale=-1.0)
                act(S, gcol[:, 0:1], gcol[:, 0:1], AF.Ln, [Bg], [Bg], bias=1.0)
                cp(S, "dve", gcol[:, 1:2], gcol[:, 5:6], [Bg], [Bg])
                bank3, bb3 = ring.next()
                mm(S, bank3[:, 0:2], mask01, gcol[:, 0:2], True, True, [Bg, Bc], [bb3], inc=False)
                mm(S, bank3[:, 2:4], ones_f, gcol[:, 0:2], True, True, [Bg, Bc], [bb3], inc=True)
                tt(S, "dve", gcol[:, 2:3], bank3[:, 0:1], gcol[:, 1:2], ALU.add, [bb3, Bg], [Bg])
                cp(S, "dve", gcol[:, 4:5], bank3[:, 0:1], [bb3], [Bg])
                cp(S, "dve", gstat[:, 3:4], bank3[:, 2:3], [bb3], [Bg])
                ts(S, "dve", diag[:], ident_f, gcol[:, 2:3], None, ALU.mult, None, [Bg, Bc], [Bg])
                bank4, bb4 = ring.next()
                mm(S, bank4[:, 0:128], ones_f, diag[:], True, True, [Bg, Bc], [bb4], inc=True)
                S.op("dve", lambda e: e.tensor_reduce(out=gstat[:, 0:1], in_=bank4[:, 0:128], axis=AX.X, op=ALU.max),
                     reads=[bb4], writes=[Bg])
                tt(S, "dve", gstat[:, 1:2], gstat[:, 0:1], gcol[:, 3:4], ALU.max, [Bg], [Bg])
                ts(S, "dve", gstat[:, 2:3], gstat[:, 1:2], -1.0, None, ALU.mult, None, [Bg], [Bg])
                act(S, Gw[:, c * 4:c * 4 + 3], gcol[:, 2:5], AF.Exp, [Bg], [BGw[c]], bias=gstat[:, 2:3])
                ts(S, "dve", Gw[:, c * 4 + 3:c * 4 + 4], Gw[:, c * 4 + 1:c * 4 + 2], 1.0 / 16.0, None, ALU.mult, None,
                   [BGw[c]], [BGw[c]])
                tt(S, "dve", gcol[:, 3:4], gstat[:, 1:2], gstat[:, 3:4], ALU.subtract, [Bg], [Bg])

            def stage2(c):
                p = c % 2
                wc = Gw[:, c * 4:c * 4 + 1]
                dec = Gw[:, c * 4 + 1:c * 4 + 2]
                bound = Gw[:, c * 4 + 2:c * 4 + 3]
                dsc = Gw[:, c * 4 + 3:c * 4 + 4]
                qT = qkT[p][:, 0:2, :]
                kT = qkT[p][:, 2:4, :]
                act(S, Cbf[:].rearrange("p a b -> p (a b)"), Cst[:].rearrange("p a b -> p (a b)"), AF.Copy,
                    [BC, BGw[c]], [BCbf], scale=dsc)
                ts(S, "dve", nbf[:], nst[:], dsc, None, ALU.mult, None, [BC, BGw[c]], [BCbf])
                bankA, bA = ring.next()
                for dt in range(2):
                    mm(S, bankA[:, 0:128], kT[:, dt, :], qT[:, dt, :], dt == 0, dt == 1, [BqkT[p]], [bA], inc=False)
                for dt in range(2):
                    mm(S, bankA[:, 128 + dt * 128:256 + dt * 128], kT[:, dt, :], ident_b, True, True,
                       [BqkT[p], Bc], [bA], inc=(dt == 1))
                stt(S, sTw[:], bankA[:, 0:128], wc, mask16, ALU.mult, ALU.mult, [bA, BGw[c], Bc], [BsTw])
                act(S, kw[:], bankA[:, 128:384], AF.Copy, [bA, BGw[c]], [Bkw], scale=wc)
                bankN, bN = ring.next()
                mm(S, bankN[:], sTw[:], vm[p][:], True, False, [BsTw, Bvm[p]], [bN], inc=False)
                for dt in range(2):
                    mm(S, bankN[:], qT[:, dt, :], Cbf[:, dt, :], False, dt == 1, [BqkT[p], BCbf], [bN], inc=(dt == 1))
                bankD, bD = ring.next()
                mm(S, bankD[:, 0:1], sTw[:], ones_b[:, 0:1], True, False, [BsTw, Bc], [bD], inc=False)
                for dt in range(2):
                    mm(S, bankD[:, 0:1], qT[:, dt, :], nbf[:, dt:dt + 1], False, dt == 1,
                       [BqkT[p], BCbf], [bD], inc=False)
                for dt in range(2):
                    mm(S, bankD[:, 2 + 2 * dt:3 + 2 * dt], kw[:, dt * 128:(dt + 1) * 128], ones_b[:, 0:1], True, True,
                       [Bkw, Bc], [bD], inc=(dt == 1))
                bU = []
                for dt in range(2):
                    bankU, bUb = ring.next()
                    mm(S, bankU[:], kw[:, dt * 128:(dt + 1) * 128], vm[p][:], True, True, [Bkw, Bvm[p]], [bUb], inc=True)
                    bU.append((bankU, bUb))
                ts(S, "dve", small[:, 14:15], bankD[:, 0:1], -1.0, None, ALU.mult, None, [bD], [Bsm])
                tt(S, "dve", small[:, 0:1], bankD[:, 0:1], small[:, 14:15], ALU.max, [bD, Bsm], [Bsm])
                ts(S, "dve", small[:, 0:1], small[:, 0:1], bound, None, ALU.max, None, [Bsm, BGw[c]], [Bsm])
                S.op("dve", lambda e: e.reciprocal(out=small[:, 1:2], in_=small[:, 0:1]), reads=[Bsm], writes=[Bsm])
                act(S, hbuf[:], bankN[:], AF.Copy, [bN, Bsm], [Bh], scale=small[:, 1:2])
                stt(S, nst[:, 0:1], nst[:, 0:1], dec, bankD[:, 2:3], ALU.mult, ALU.add, [BC, BGw[c], bD, BCbf], [BC])
                stt(S, nst[:, 1:2], nst[:, 1:2], dec, bankD[:, 4:5], ALU.mult, ALU.add, [BC, BGw[c], bD, BCbf], [BC])
                for dt in range(2):
                    stt(S, Cst[:, dt, :], Cst[:, dt, :], dec, bU[dt][0][:], ALU.mult, ALU.add,
                        [BC, BGw[c], bU[dt][1], BCbf], [BC])
                layer_norm_rows(S, hbuf[:], Bh, 512, small[:, 2:8], small[:, 8:10], small[:, 10:11], small[:, 11:12],
                                small[:, 12:13], Bsm)
                tt(S, "pool", hm[:], hbuf[:], gm[p][:], ALU.mult, [Bh, Bgm[p]], [Bhm])
                bankT, bT = ring.next()
                for j in range(4):
                    mm(S, bankT[:, j * 128:(j + 1) * 128], hm[:, j * 128:(j + 1) * 128], ident_b, True, True,
                       [Bhm, Bc], [bT], inc=(j == 3))
                cp(S, "dve", hoT[p][:].rearrange("p a b -> p (a b)"), bankT[:], [bT], [BhoT[p]])
                dma(S, "sp", ex_dst(c, 0), hoT[p][:], [BhoT[p]], [Bex])
                bankG, bG = ring.next()
                for v in range(4):
                    mm(S, bankG[:, v * 128:(v + 1) * 128], rqs[p][:, v, :], ident_b, True, True, [Brqs[p], Bc], [bG],
                       inc=(v == 3))
                cp(S, "act", rT[:].rearrange("p a b -> p (a b)"), bankG[:], [bG], [BrT])
                bankS, bS = ring.next()
                for h in range(2):
                    mm(S, bankS[:, h * 128:(h + 1) * 128], rT[:, 2 + h, :], rT[:, h, :], True, True, [BrT], [bS],
                       inc=(h == 1))
                for h in range(2):
                    tt(S, "dve", sTm[:, h, :], bankS[:, h * 128:(h + 1) * 128], mask01, ALU.mult, [bS, Bc], [BsTm])
                for h in range(2):
                    act(S, Zbf[:, h, :], Zst[:, h, :], AF.Copy, [BZ, Bc], [BZbf], scale=rconst[:, 4 + h:5 + h])
                bankO, bO = ring.next()
                for h in range(2):
                    mm(S, bankO[:, h * 256:(h + 1) * 256], sTm[:, h, :], rv[p][:, h * 256:(h + 1) * 256], True, False,
                       [BsTm, Brv[p]], [bO], inc=False)
                    mm(S, bankO[:, h * 256:(h + 1) * 256], rT[:, h, :], Zbf[:, h, :], False, True,
                       [BrT, BZbf], [bO], inc=(h == 1))
                bankR, bR = ring.next()
                for h in range(2):
                    mm(S, bankR[:, h * 256:(h + 1) * 256], rqs[p][:, 2 + h, :], rv[p][:, h * 256:(h + 1) * 256], True, True,
                       [Brqs[p], Brv[p]], [bR], inc=(h == 1))
                cp(S, "act", obuf[:], bankO[:], [bO], [Bo])
                for h in range(2):
                    stt(S, Zst[:, h, :], Zst[:, h, :], rconst[:, 4 + h:5 + h], bankR[:, h * 256:(h + 1) * 256],
                        ALU.mult, ALU.add, [BZ, Bc, bR, BZbf], [BZ])
                for h in range(2):
                    o_h = obuf[:, h * 256:(h + 1) * 256]
                    S.op("dve", lambda e, o_h=o_h: e.bn_stats(out=rsm[:, 0:6], in_=o_h), reads=[Bo], writes=[Brsm])
                    S.op("dve", lambda e: e.bn_aggr(out=rsm[:, 8:10], in_=rsm[:, 0:6]), reads=[Brsm], writes=[Brsm])
                    act(S, rsm[:, 10:11], rsm[:, 9:10], AF.Sqrt, [Brsm], [Brsm], bias=EPS)
                    S.op("dve", lambda e: e.reciprocal(out=rsm[:, 11:12], in_=rsm[:, 10:11]), reads=[Brsm], writes=[Brsm])
                    ts(S, "dve", rsm[:, 12:13], rsm[:, 8:9], -1.0, rsm[:, 11:12], ALU.mult, ALU.mult, [Brsm], [Brsm])
                    act(S, o_h, o_h, AF.Identity, [Bo, Brsm], [Bo], bias=rsm[:, 12:13], scale=rsm[:, 11:12])
                tt(S, "pool", hr[:], obuf[:], grg[p][:], ALU.mult, [Bo, Bgrg[p]], [Bhr])
                bankT2, bT2 = ring.next()
                for j in range(4):
                    mm(S, bankT2[:, j * 128:(j + 1) * 128], hr[:, j * 128:(j + 1) * 128], ident_b, True, True,
                       [Bhr, Bc], [bT2], inc=(j == 3))
                cp(S, "act", hrT[p][:].rearrange("p a b -> p (a b)"), bankT2[:], [bT2], [BhrT[p]])
                dma(S, "sp", ex_dst(c, 1), hrT[p][:], [BhrT[p]], [Bex])

            load_x(0)
            load_x(1)
            for c in range(nch_run + 1):
                if c < nch_run:
                    stage1(c)
                if c >= 1:
                    stage2(c - 1)
            S.wait_all("sp", [Bex])
            S.run()
            return nc

        Bexo = Buf("unused")
        Bxc = [Buf(f"xc{i}") for i in range(2)]

        pb = Bump(COMMON_END)
        NSLOT = 7
        HOLD = 4
        RAo = pb.off
        xTb = pb("xTb", [128, KT, TQ], BF16)
        hmTb = pb("hmTb", [128, KT, TQ], BF16)
        r1 = arena[:, RAo:RAo + NTT * D].rearrange("p (t n) -> p t n", t=NTT)
        RBo = pb.off
        hrTb = pb("hrTb", [128, KT, TQ], BF16)
        h1T = arena[:, RBo:RBo + KT * TQ // 2].bitcast(BF16).rearrange("p (k n) -> p k n", k=KT)
        RCo = pb.off
        mgT = pb("mgT", [128, KT, TQ], BF16)
        hTb = arena[:, RCo:RCo + 4 * TQ // 2].bitcast(BF16).rearrange("p (k n) -> p k n", k=4)
        rowt = [pb(f"rowt{i}", [128, D]) for i in range(3)]
        tmpa = pb("tmpa", [128, 512])
        tmpb = pb("tmpb", [128, 512])
        rl = pb("rl", [128, 512])
        wslot = [pb(f"wslot{i}", [128, 2048], BF16) for i in range(NSLOT)]
        Bslot = [Buf(f"wslot{i}") for i in range(NSLOT)]
        BxTb, BhmTb, BhrTb, BmgT = Buf("xTb"), Buf("hmTb"), Buf("hrTb"), Buf("mgT")
        Br1 = [Buf(f"r1_{t}") for t in range(NTT)]
        Bh1T = Buf("h1T")
        BhT = [Buf(f"hT{m}") for m in range(4)]
        Bta, Btb, Brl, Bln = Buf("tmpa"), Buf("tmpb"), Buf("rl"), Buf("lnst")

        dma(S, "sp", colsb[:], colsB, [], [Bc])
        bgate = colsb[:, 0:32]
        bff1 = colsb[:, 32:96]
        Brow = [Buf(f"rowt{i}") for i in range(3)]

        def load_row(slot, r):
            dma(S, "sp", rowt[slot][:], rowsB[r:r + 1, :].broadcast_to([128, D]), [], [Brow[slot]])

        wstate = {"issued": 0}

        def w_issue_upto(n):
            while wstate["issued"] < min(n, NWB):
                i = wstate["issued"]
                s = i % NSLOT
                S.dma("pool", lambda e, i=i, s=s: e.dma_start(out=wslot[s], in_=wB_block(i)),
                      reads=([Bexo] if i >= NWD else []), writes=[Bslot[s]])
                wstate["issued"] += 1

        def wget(i):
            w_issue_upto(i + NSLOT - HOLD + 1)
            assert wstate["issued"] > i
            s = i % NSLOT
            return wslot[s], Bslot[s]

        import os as _os
        if "wpre" not in _os.environ.get("KSKIP", ""):
            w_issue_upto(NSLOT - HOLD + 1)
        load_row(0, 0)
        load_row(1, 1)
        load_row(2, 2)

        def ld_xq(e, dst, t):
            return e.dma_start(out=dst, in_=xq[t * 128:(t + 1) * 128, :])

        HR = arena[:, RAo + KT * TQ // 2:RAo + 3 * KT * TQ // 2].bitcast(BF16).rearrange(
            "p (k n) -> p k n", k=2 * KT)

        def hm_k(k):
            return HR[:, (k // 4) * 8 + k % 4, :]

        def hr_k(k):
            return HR[:, (k // 4) * 8 + 4 + k % 4, :]

        for r in range(4):
            dma(S, "sp", HR[:, r * 8:(r + 1) * 8, :],
                hmhr[r * 1024:(r + 1) * 1024, :].rearrange("(j p) t -> p j t", p=128), [], [BhmTb, BhrTb])
        for t in range(NTT):
            p = t % 2
            S.dma("sp", lambda e, p=p, t=t: ld_xq(e, xc[p][:], t), writes=[Bxc[p]])
            for j in range(4):
                bank, bb = ring.next()
                for i in range(4):
                    k = 4 * j + i
                    tr(S, bank[:, i * 128:(i + 1) * 128], xc[p][:, k * 128:(k + 1) * 128], ident_f,
                       [Bxc[p], Bc], [bb], inc=(i == 3))
                for i in range(4):
                    k = 4 * j + i
                    eng = "act" if i % 2 == 0 else "dve"
                    cp(S, eng, xTb[:, k, t * 128:(t + 1) * 128], bank[:, i * 128:(i + 1) * 128], [bb], [BxTb])

        if stop_after == 'B1':
            S.wait_all("sp", [])
            S.barrier()
            S.run()
            return nc
        for d in range(KT):
            res = []
            xk = lambda k: xTb[:, k, :]
            for qi, (actT, Bact) in enumerate(((xk, BxTb), (xk, BxTb), (hm_k, BhmTb), (hr_k, BhrTb))):
                wsl, Bw = wget(d * 4 + qi)
                w3 = wsl.rearrange("p (k n) -> p k n", k=KT)
                banks = [ring.next(), ring.next()]
                for k in range(KT):
                    for half in range(2):
                        mm(S, banks[half][0][:], w3[:, k, :], actT(k)[:, half * 512:(half + 1) * 512],
                           k == 0, k == KT - 1, [Bw, Bact], [banks[half][1]], inc=(k == KT - 1))
                res.append(banks)
            for half in range(2):
                act(S, tmpa[:], res[0][half][0][:], AF.Sigmoid, [res[0][half][1], Bc], [Bta],
                    bias=bgate[:, d:d + 1])
                act(S, tmpb[:], res[1][half][0][:], AF.Sigmoid, [res[1][half][1], Bc], [Btb],
                    bias=bgate[:, 16 + d:17 + d])
                tt(S, "dve", tmpa[:], res[2][half][0][:], tmpa[:], ALU.mult,
                   [res[2][half][1], Bta], [Bta])
                tt(S, "dve", tmpb[:], res[3][half][0][:], tmpb[:], ALU.mult,
                   [res[3][half][1], Btb], [Btb])
                tt(S, "dve", mgT[:, d, half * 512:(half + 1) * 512], tmpa[:], tmpb[:], ALU.add,
                   [Bta, Btb], [BmgT])

        if stop_after == 'B2':
            S.wait_all("sp", [])
            S.barrier()
            S.run()
            return nc
        S.barrier()

        for cg in range(4):
            wblk = [wget(64 + cg * 4 + kq) for kq in range(4)]
            for t in range(NTT):
                bank, bb = ring.next()
                for k in range(KT):
                    wsl, Bw = wblk[k // 4]
                    w3 = wsl.rearrange("p (k n) -> p k n", k=4)
                    mm(S, bank[:], mgT[:, k, t * 128:(t + 1) * 128], w3[:, k % 4, :], k == 0, k == KT - 1,
                       [BmgT, Bw], [bb], inc=(k == KT - 1))
                tt(S, "dve", r1[:, t, cg * 512:(cg + 1) * 512], bank[:], rowt[0][:, cg * 512:(cg + 1) * 512], ALU.add,
                   [bb, Brow[0]], [Br1[t]])
        for t in range(NTT):
            p = t % 2
            S.dma("sp", lambda e, p=p, t=t: ld_xq(e, xc[p][:], t), writes=[Bxc[p]])
            stt(S, r1[:, t, :], xc[p][:], ALPHA, r1[:, t, :], ALU.mult, ALU.add, [Bxc[p], Br1[t]], [Br1[t]])
            layer_norm_rows(S, r1[:, t, :], Br1[t], D, lnst[:, 0:24], lnst[:, 24:26], lnst[:, 26:27], lnst[:, 27:28],
                            lnst[:, 28:29], Bln)
            tt(S, "dve", r1[:, t, :], r1[:, t, :], rowt[1][:], ALU.mult, [Br1[t], Brow[1]], [Br1[t]])
            tt(S, "dve", r1[:, t, :], r1[:, t, :], rowt[2][:], ALU.add, [Br1[t], Brow[2]], [Br1[t]])
            if t == NTT - 1:
                pass
            for j in range(4):
                bank, bb = ring.next()
                for i in range(4):
                    k = 4 * j + i
                    tr(S, bank[:, i * 128:(i + 1) * 128], r1[:, t, k * 128:(k + 1) * 128], ident_f,
                       [Br1[t], Bc], [bb], inc=(i == 3))
                for i in range(4):
                    k = 4 * j + i
                    eng = "act" if i % 2 == 0 else "dve"
                    cp(S, eng, h1T[:, k, t * 128:(t + 1) * 128], bank[:, i * 128:(i + 1) * 128], [bb], [Bh1T])
        if stop_after == 'B3':
            S.wait_all("sp", [])
            S.barrier()
            S.run()
            return nc
        load_row(0, 3)
        for t in range(NTT):
            stt(S, r1[:, t, :], r1[:, t, :], ALPHA, rowt[0][:], ALU.mult, ALU.add, [Br1[t], Brow[0]], [Br1[t]])
        load_row(1, 4)
        load_row(2, 5)

        for fb in range(16):
            base = 80 + fb * 8
            for m in range(4):
                wsl, Bw = wget(base + m)
                w3 = wsl.rearrange("p (k n) -> p k n", k=KT)
                banks = [ring.next(), ring.next()]
                for k in range(KT):
                    for half in range(2):
                        mm(S, banks[half][0][:], w3[:, k, :], h1T[:, k, half * 512:(half + 1) * 512],
                           k == 0, k == KT - 1, [Bw, Bh1T], [banks[half][1]], inc=(k == KT - 1))
                for half in range(2):
                    act(S, rl[:], banks[half][0][:], AF.Relu, [banks[half][1], Bc], [Brl],
                        bias=bff1[:, fb * 4 + m:fb * 4 + m + 1])
                    tt(S, "dve", hTb[:, m, half * 512:(half + 1) * 512], rl[:], rl[:], ALU.mult,
                       [Brl], [BhT[m]])
            w2 = [wget(base + 4 + kt) for kt in range(4)]
            for t in range(NTT):
                for cg in range(4):
                    bank, bb = ring.next()
                    for kt in range(4):
                        mm(S, bank[:], hTb[:, kt, t * 128:(t + 1) * 128], w2[kt][0][:, cg * 512:(cg + 1) * 512],
                           kt == 0, kt == 3, [BhT[kt], w2[kt][1]], [bb], inc=(kt == 3))
                    tt(S, "dve", r1[:, t, cg * 512:(cg + 1) * 512], bank[:], r1[:, t, cg * 512:(cg + 1) * 512], ALU.add,
                       [bb, Br1[t]], [Br1[t]])

        if stop_after == 'B4':
            S.wait_all("sp", [])
            S.barrier()
            S.run()
            return nc
        Bout = Buf("out")
        for t in range(NTT):
            layer_norm_rows(S, r1[:, t, :], Br1[t], D, lnst[:, 0:24], lnst[:, 24:26], lnst[:, 26:27], lnst[:, 27:28],
                            lnst[:, 28:29], Bln)
            tt(S, "dve", r1[:, t, :], r1[:, t, :], rowt[1][:], ALU.mult, [Br1[t], Brow[1]], [Br1[t]])
            tt(S, "dve", r1[:, t, :], r1[:, t, :], rowt[2][:], ALU.add, [Br1[t], Brow[2]], [Br1[t]])
            dma(S, "sp", out[t * 128:(t + 1) * 128, :], r1[:, t, :], [Br1[t]], [Bout])
        S.wait_all("sp", [Bout])
        S.run()
    return nc


SPLITS = (1024, 1024, 2048, 2048, 4, 4, 1024, 1024, 2048, 2048, 2048, 2048)
OFFS = np.concatenate([[0], np.cumsum(SPLITS)]).astype(int)


def _const_mats():
    ident = np.eye(128, dtype=np.float32)
    idx = np.arange(128)
    m01 = (idx[:, None] <= idx[None, :]).astype(np.float32)
    m16 = m01 / 16.0
    ones = np.ones((128, 128), np.float32)
    return np.ascontiguousarray(np.concatenate([ident, m16, m01, ones], axis=1))


def _host_inputs(inp):
    x = np.asarray(inp["x"], np.float32)
    positions = np.asarray(inp["positions"], np.int32)
    w_in = np.asarray(inp["w_in"], np.float32)[0]
    b_in = np.asarray(inp["b_in"], np.float32)[0]
    conv_w = np.asarray(inp["m_conv_w"], np.float32)[0]
    conv_b = np.asarray(inp["m_conv_b"], np.float32)[0]
    m_norm_g = np.asarray(inp["m_norm_g"], np.float32)[0]
    r_norm_g = np.asarray(inp["r_norm_g"], np.float32)[0]
    w_bm = np.asarray(inp["w_branch_m"], np.float32)[0]
    w_br = np.asarray(inp["w_branch_r"], np.float32)[0]
    w_out = np.asarray(inp["w_out"], np.float32)[0]
    w_ff1 = np.asarray(inp["w_ff1"], np.float32)[0]
    w_ff2 = np.asarray(inp["w_ff2"], np.float32)[0]
    o = OFFS
    cmat = _const_mats()
    half = 64
    inv_freq = (10000.0 ** (-np.arange(half, dtype=np.float32) / half)).astype(np.float32)
    invf = np.ascontiguousarray(np.broadcast_to(inv_freq[None, :], (128, 64))).astype(np.float32)

    def fm_block(w, d):
        return w[:, d * 128:(d + 1) * 128].reshape(16, 128, 128).transpose(1, 0, 2).reshape(128, 2048)

    w_ga = w_in[:, o[10]:o[11]]
    w_gb = w_in[:, o[11]:o[12]]
    blocks = []
    for d in range(16):
        blocks += [fm_block(w_ga, d), fm_block(w_gb, d), fm_block(w_bm, d), fm_block(w_br, d)]
    for cg in range(4):
        for kq in range(4):
            blk = w_out[kq * 512:(kq + 1) * 512, cg * 512:(cg + 1) * 512]
            blocks.append(blk.reshape(4, 128, 512).transpose(1, 0, 2).reshape(128, 2048))
    for fb in range(16):
        for m in range(4):
            blocks.append(fm_block(w_ff1, fb * 4 + m))
        for kt in range(4):
            f0 = fb * 512 + kt * 128
            blocks.append(w_ff2[f0:f0 + 128, :])
    wB = np.ascontiguousarray(np.stack(blocks, axis=0)).astype(np.float32)
    assert wB.shape == (NWB, 128, 2048)
    colsB = np.zeros((128, 96), np.float32)
    colsB[:, 0:16] = b_in[o[10]:o[11]].reshape(16, 128).T
    colsB[:, 16:32] = b_in[o[11]:o[12]].reshape(16, 128).T
    colsB[:, 32:96] = np.asarray(inp["b_ff1"], np.float32)[0].reshape(64, 128).T
    rowsB = np.stack([np.asarray(inp[k], np.float32)[0] for k in
                      ("b_out", "ln1_g", "ln1_b", "b_ff2", "ln2_g", "ln2_b")], axis=0)

    mapsA, mapsB = [], []
    wBflat = np.ascontiguousarray(wB.reshape(-1, 2048))
    for c in range(8):
        b, g = c // 4, c % 4
        cols_idx = np.concatenate([
            np.arange(o[0] + g * 256, o[0] + (g + 1) * 256),
            np.arange(o[1] + g * 256, o[1] + (g + 1) * 256),
            np.arange(o[2] + g * 512, o[2] + (g + 1) * 512),
            np.arange(o[3] + g * 512, o[3] + (g + 1) * 512),
            np.arange(o[6] + 2 * g * 128, o[6] + (2 * g + 2) * 128),
            np.arange(o[7] + 2 * g * 128, o[7] + (2 * g + 2) * 128),
            np.arange(o[8] + 2 * g * 256, o[8] + (2 * g + 2) * 256),
            np.arange(o[9] + 2 * g * 256, o[9] + (2 * g + 2) * 256),
            np.array([o[4] + g, o[5] + g]),
        ])
        wA = np.ascontiguousarray(w_in[:, cols_idx])
        bsel = b_in[cols_idx]
        colsA = np.zeros((128, 40), np.float32)
        colsA[:, 0:4] = bsel[0:512].reshape(4, 128).T
        colsA[:, 4] = bsel[3072]
        colsA[:, 5] = bsel[3073]
        cch = np.concatenate([np.arange(g * 256, (g + 1) * 256), np.arange(1024 + g * 256, 1024 + (g + 1) * 256)])
        cw = conv_w[:, cch]
        colsA[:, 6:22] = cw.reshape(4, 4, 128).transpose(2, 1, 0).reshape(128, 16)
        colsA[:, 22:26] = conv_b[cch].reshape(4, 128).T
        tix = np.arange(128, dtype=np.float64)
        for hh in range(2):
            hidx = 2 * g + hh
            gamma = 1.0 - 2.0 ** (-5.0 - hidx)
            lg = math.log(gamma)
            colsA[:, 26 + hh] = np.exp((tix - 128.0) * lg) / math.sqrt(128.0)
            colsA[:, 28 + hh] = np.exp((128.0 - tix) * lg)
            colsA[:, 30 + hh] = math.exp(128.0 * lg)
        mapsA.append({
            "cmat": cmat,
            "xb": np.ascontiguousarray(x[b]),
            "pos": np.ascontiguousarray(positions[b].reshape(NCH, 128).T),
            "wA": wA,
            "bA_tm": np.ascontiguousarray(bsel[512:3072].reshape(1, 2560)),
            "colsA": colsA,
            "mng": np.ascontiguousarray(m_norm_g[g].reshape(1, 512)),
            "rng": np.ascontiguousarray(r_norm_g[2 * g:2 * g + 2].reshape(1, 512)),
            "invf": invf,
        })
        mapsB.append({
            "cmat": cmat,
            "xq": np.ascontiguousarray(x[b, g * TQ:(g + 1) * TQ]),
            "wBd": wBflat,
            "colsB": colsB,
            "rowsB": np.ascontiguousarray(rowsB),
        })
    return mapsA, mapsB


_NC_CACHE = {}


def kernel(**inputs):
    mapsA, mapsB = _host_inputs(inputs)
    if "A" not in _NC_CACHE:
        _NC_CACHE["A"] = build_program("A")
        _NC_CACHE["B"] = build_program("B")
    resA = run_bass_kernel_spmd(_NC_CACHE["A"], mapsA, core_ids=list(range(8)))
    hx = [np.asarray(resA.results[c]["hx"]) for c in range(8)]
    for c in range(8):
        b, q = c // 4, c % 4
        mapsB[c]["hmhr"] = np.ascontiguousarray(
            np.concatenate([hx[b * 4 + r][:, q * TQ:(q + 1) * TQ] for r in range(4)], axis=0))
    resB = run_bass_kernel_spmd(_NC_CACHE["B"], mapsB, core_ids=list(range(8)))
    outp = np.zeros((2, T, D), np.float32)
    for c in range(8):
        b, g = c // 4, c % 4
        outp[b, g * TQ:(g + 1) * TQ] = resB.results[c]["out"]
    return outp
```
